# Optimizing a Trainium2 kernel written in Bass

```python
import math
import jax, jax.numpy as jnp
from jax import lax
import numpy as np

D_MODEL = 1024
BATCH = 8
SEQ = 2048
DEPTH = 2
DEC_BATCH = 128
DEC_SEQ = 8
PAST_LEN = 16384
PAGE_SIZE = 128

N_PAIRS = DEPTH // 2
BRANCH = D_MODEL
MIX_WIDTH = 2 * BRANCH
A_CONV = 3
B_CONV = 31
N_HEADS_C = 4
N_HEADS_D = 4
HEAD_DIM = BRANCH // N_HEADS_C
CHUNK = 128
ROPE_BASE = 10000.0
EPS = 1e-6
CONV_IN_COLS = 7 * BRANCH
REC_IN_COLS = 9 * BRANCH + 2 * N_HEADS_C

kernel_name = 'hybrid_conv_mlstm_retention_step'


def rmsnorm(x, g):
    xf = x.astype(jnp.float32)
    y = xf * lax.rsqrt(jnp.mean(xf * xf, axis=-1, keepdims=True) + EPS)
    return (y * g.astype(jnp.float32)).astype(x.dtype)


def layernorm(x, g, b):
    xf = x.astype(jnp.float32)
    mu = jnp.mean(xf, axis=-1, keepdims=True)
    var = jnp.mean(jnp.square(xf - mu), axis=-1, keepdims=True)
    y = (xf - mu) * lax.rsqrt(var + EPS)
    return (y * g.astype(jnp.float32) + b.astype(jnp.float32)).astype(x.dtype)


def headnorm(x, g, n_heads):
    bsz, t, _ = x.shape
    xs = x.reshape(bsz, t, n_heads, -1)
    mu = jnp.mean(xs, axis=-1, keepdims=True)
    var = jnp.mean(jnp.square(xs - mu), axis=-1, keepdims=True)
    y = ((xs - mu) * lax.rsqrt(var + EPS)).reshape(bsz, t, -1)
    return y * g.astype(jnp.float32)


def causal_dwconv(u, buf, w):
    xp = jnp.concatenate([buf.astype(u.dtype), u], axis=1)
    width, ch = w.shape
    out = lax.conv_general_dilated(xp, w.astype(u.dtype)[:, None, :], window_strides=(1,), padding='VALID',
                                   dimension_numbers=('NWC', 'WIO', 'NWC'), feature_group_count=ch)
    return out, xp[:, xp.shape[1] - (width - 1):]


def split_heads(x, n_heads):
    bsz, t, _ = x.shape
    return x.reshape(bsz, t, n_heads, -1).transpose(0, 2, 1, 3)


def merge_heads(x):
    bsz, h, t, d = x.shape
    return x.transpose(0, 2, 1, 3).reshape(bsz, t, h * d)


def to_chunks(a, n_chunks, length):
    a = a.reshape(a.shape[:2] + (n_chunks, length) + a.shape[3:])
    return jnp.moveaxis(a, 2, 0)


def from_chunks(a):
    a = jnp.moveaxis(a, 0, 2)
    return a.reshape(a.shape[:2] + (-1,) + a.shape[4:])


def rope(x, pos):
    half = x.shape[-1] // 2
    inv = ROPE_BASE ** (-jnp.arange(half, dtype=jnp.float32) / half)
    ang = pos[:, None] * inv[None, :]
    cos, sin = jnp.cos(ang), jnp.sin(ang)
    x1, x2 = x[..., :half], x[..., half:]
    return jnp.concatenate([x1 * cos - x2 * sin, x2 * cos + x1 * sin], axis=-1)


def mlstm_chunked(q, k, v, i_pre, log_f, c0, n0, m0):
    t = q.shape[2]
    length = math.gcd(t, CHUNK)
    nc = t // length
    tril = jnp.tril(jnp.ones((length, length), dtype=bool))

    def step(carry, xs):
        c, n, m = carry
        qc, kc, vc, ic, fc = xs
        b = jnp.cumsum(fc, axis=-1)
        dmat = jnp.where(tril, b[..., :, None] - b[..., None, :] + ic[..., None, :], -jnp.inf)
        inter = b + m[..., None]
        m_t = jnp.maximum(jnp.max(dmat, axis=-1), inter)
        w_intra = jnp.exp(dmat - m_t[..., None])
        w_inter = jnp.exp(inter - m_t)
        s = jnp.einsum('bhtd,bhsd->bhts', qc, kc) * w_intra
        num = jnp.einsum('bhts,bhsv->bhtv', s, vc) + w_inter[..., None] * jnp.einsum('bhtd,bhdv->bhtv', qc, c)
        den = jnp.sum(s, axis=-1) + w_inter * jnp.einsum('bhtd,bhd->bht', qc, n)
        h = num / jnp.maximum(jnp.abs(den), jnp.exp(-m_t))[..., None]
        b_last = b[..., -1]
        dec_s = b_last[..., None] - b + ic
        m_new = jnp.maximum(b_last + m, jnp.max(dec_s, axis=-1))
        w_s = jnp.exp(dec_s - m_new[..., None])
        scale = jnp.exp(b_last + m - m_new)
        c_new = scale[..., None, None] * c + jnp.einsum('bhs,bhsd,bhsv->bhdv', w_s, kc, vc)
        n_new = scale[..., None] * n + jnp.einsum('bhs,bhsd->bhd', w_s, kc)
        return (c_new, n_new, m_new), h

    xs = tuple(to_chunks(a, nc, length) for a in (q, k, v, i_pre, log_f))
    (c, n, m), hs = lax.scan(step, (c0, n0, m0), xs)
    return from_chunks(hs), c, n, m


def retention_chunked(q, k, v, s0):
    n_heads, t = q.shape[1], q.shape[2]
    length = math.gcd(t, CHUNK)
    nc = t // length
    log_g = jnp.log(1.0 - 2.0 ** (-5.0 - jnp.arange(n_heads, dtype=jnp.float32)))
    idx = jnp.arange(length, dtype=jnp.float32)
    rel = idx[:, None] - idx[None, :]
    dmask = jnp.where(rel >= 0, jnp.exp(log_g[:, None, None] * jnp.maximum(rel, 0.0)), 0.0)
    q_dec = jnp.exp(log_g[:, None] * (idx + 1.0))[:, :, None]
    k_dec = jnp.exp(log_g[:, None] * (length - 1.0 - idx))[:, :, None]
    g_len = jnp.exp(log_g * length)[:, None, None]

    def step(s_state, xs):
        qc, kc, vc = xs
        sc = jnp.einsum('bhtd,bhsd->bhts', qc, kc) * dmask
        o = jnp.einsum('bhts,bhsv->bhtv', sc, vc) + q_dec * jnp.einsum('bhtd,bhdv->bhtv', qc, s_state)
        s_new = g_len * s_state + jnp.einsum('bhsd,bhsv->bhdv', kc * k_dec, vc)
        return s_new, o

    xs = tuple(to_chunks(a, nc, length) for a in (q, k, v))
    s_fin, os_ = lax.scan(step, s0, xs)
    return from_chunks(os_), s_fin


def conv_mixer(h, buf_a, buf_b, w_in, a_w, b_w, b_bias, b_g, b_b, w_out):
    proj = h @ w_in
    a_bg, a_cg, a_x, a_z, b_val, b_gate, b_z = jnp.split(proj, 7, axis=-1)
    conv_a, new_a = causal_dwconv(a_cg * a_x, buf_a, a_w)
    y_a = a_bg * conv_a * jax.nn.silu(a_z)
    conv_b, new_b = causal_dwconv(b_val * jax.nn.sigmoid(b_gate), buf_b, b_w)
    y_b = jax.nn.silu(layernorm(conv_b + b_bias, b_g, b_b)) * jax.nn.silu(b_z)
    return jnp.concatenate([y_a, y_b], axis=-1) @ w_out, new_a, new_b


def recurrent_mixer(h, pos, c0, n0, m0, s0, w_in, c_i_b, c_f_b, c_hn_g, d_hn_g, w_out):
    f32 = jnp.float32
    proj = h @ w_in
    c_q, c_k, c_v, c_o, c_z, d_q, d_k, d_v, d_z = jnp.split(proj[..., :9 * BRANCH], 9, axis=-1)
    gates = proj[..., 9 * BRANCH:].astype(f32)
    i_pre = (gates[..., :N_HEADS_C] + c_i_b.astype(f32)).transpose(0, 2, 1)
    log_f = jax.nn.log_sigmoid(gates[..., N_HEADS_C:] + c_f_b.astype(f32)).transpose(0, 2, 1)
    scale = HEAD_DIM ** -0.5
    hc, c_new, n_new, m_new = mlstm_chunked(
        split_heads(c_q, N_HEADS_C).astype(f32), split_heads(c_k, N_HEADS_C).astype(f32) * scale,
        split_heads(c_v, N_HEADS_C).astype(f32), i_pre, log_f,
        c0.astype(f32), n0.astype(f32), m0.astype(f32))
    h_c = jax.nn.sigmoid(c_o.astype(f32)) * merge_heads(hc)
    y_c = headnorm(h_c, c_hn_g, N_HEADS_C) * jax.nn.silu(c_z.astype(f32))
    qd = rope(split_heads(d_q, N_HEADS_D).astype(f32), pos)
    kd = rope(split_heads(d_k, N_HEADS_D).astype(f32), pos) * scale
    hd, s_new = retention_chunked(qd, kd, split_heads(d_v, N_HEADS_D).astype(f32), s0.astype(f32))
    y_d = headnorm(merge_heads(hd), d_hn_g, N_HEADS_D) * jax.nn.silu(d_z.astype(f32))
    y = jnp.concatenate([y_c, y_d], axis=-1).astype(h.dtype) @ w_out
    return y, c_new, n_new, m_new, s_new


def setup_inputs(seed: int = 0) -> dict:
    key = jax.random.key(seed)
    ks = jax.random.split(key, 24)

    def nrm(k, shape, s):
        return jax.random.normal(k, shape, jnp.float32) * s

    return {
        'x_prompt': nrm(ks[0], (BATCH, SEQ, D_MODEL), 1.0),
        'x_sample': nrm(ks[1], (DEC_BATCH, DEC_SEQ, D_MODEL), 1.0),
        'state_a_conv': nrm(ks[2], (N_PAIRS, DEC_BATCH, A_CONV - 1, BRANCH), 1.0),
        'state_b_conv': nrm(ks[3], (N_PAIRS, DEC_BATCH, B_CONV - 1, BRANCH), 0.5),
        'state_c_C': nrm(ks[4], (N_PAIRS, DEC_BATCH, N_HEADS_C, HEAD_DIM, HEAD_DIM), 0.5),
        'state_c_n': nrm(ks[5], (N_PAIRS, DEC_BATCH, N_HEADS_C, HEAD_DIM), 0.5),
        'state_c_m': nrm(ks[6], (N_PAIRS, DEC_BATCH, N_HEADS_C), 1.0),
        'state_d_S': nrm(ks[7], (N_PAIRS, DEC_BATCH, N_HEADS_D, HEAD_DIM, HEAD_DIM), 0.5),
        'norm_pre': 1.0 + nrm(ks[8], (DEPTH, D_MODEL), 0.1),
        'norm_post': 1.0 + nrm(ks[9], (DEPTH, D_MODEL), 0.1),
        'w_in_conv': nrm(ks[10], (N_PAIRS, D_MODEL, CONV_IN_COLS), D_MODEL ** -0.5),
        'a_conv_w': nrm(ks[11], (N_PAIRS, A_CONV, BRANCH), A_CONV ** -0.5),
        'b_conv_w': nrm(ks[12], (N_PAIRS, B_CONV, BRANCH), B_CONV ** -0.5),
        'b_conv_b': nrm(ks[13], (N_PAIRS, BRANCH), 0.02),
        'b_ln_g': 1.0 + nrm(ks[14], (N_PAIRS, BRANCH), 0.1),
        'b_ln_b': nrm(ks[15], (N_PAIRS, BRANCH), 0.02),
        'w_out_conv': nrm(ks[16], (N_PAIRS, MIX_WIDTH, D_MODEL), MIX_WIDTH ** -0.5),
        'w_in_rec': nrm(ks[17], (N_PAIRS, D_MODEL, REC_IN_COLS), D_MODEL ** -0.5),
        'c_i_b': nrm(ks[18], (N_PAIRS, N_HEADS_C), 0.1),
        'c_f_b': jnp.linspace(3.0, 6.0, N_HEADS_C, dtype=jnp.float32)[None, :] + nrm(ks[19], (N_PAIRS, N_HEADS_C), 0.1),
        'c_hn_g': 1.0 + nrm(ks[20], (N_PAIRS, BRANCH), 0.1),
        'd_hn_g': 1.0 + nrm(ks[21], (N_PAIRS, BRANCH), 0.1),
        'w_out_rec': nrm(ks[22], (N_PAIRS, MIX_WIDTH, D_MODEL), MIX_WIDTH ** -0.5),
    }


def reference(x_prompt, x_sample, state_a_conv, state_b_conv, state_c_C, state_c_n, state_c_m, state_d_S,
              norm_pre, norm_post, w_in_conv, a_conv_w, b_conv_w, b_conv_b, b_ln_g, b_ln_b, w_out_conv,
              w_in_rec, c_i_b, c_f_b, c_hn_g, d_hn_g, w_out_rec):
    bp, tp, _ = x_prompt.shape
    ts = x_sample.shape[1]
    dt = x_prompt.dtype
    pos_p = jnp.arange(tp, dtype=jnp.float32)
    pos_s = PAST_LEN + jnp.arange(ts, dtype=jnp.float32)
    zero_a = jnp.zeros((bp, A_CONV - 1, BRANCH), dt)
    zero_b = jnp.zeros((bp, B_CONV - 1, BRANCH), dt)
    zero_c = jnp.zeros((bp, N_HEADS_C, HEAD_DIM, HEAD_DIM), jnp.float32)
    zero_n = jnp.zeros((bp, N_HEADS_C, HEAD_DIM), jnp.float32)
    zero_m = jnp.zeros((bp, N_HEADS_C), jnp.float32)
    zero_s = jnp.zeros((bp, N_HEADS_D, HEAD_DIM, HEAD_DIM), jnp.float32)

    a_p, a_s, b_p, b_s, cc_p, cc_s, cn_p, cn_s, cm_p, cm_s, ds_p, ds_s = ([] for _ in range(12))
    xp, xs = x_prompt, x_sample
    for layer in range(DEPTH):
        p = layer // 2
        hp = rmsnorm(xp, norm_pre[layer])
        hs = rmsnorm(xs, norm_pre[layer])
        if layer % 2 == 0:
            wts = (w_in_conv[p], a_conv_w[p], b_conv_w[p], b_conv_b[p], b_ln_g[p], b_ln_b[p], w_out_conv[p])
            out_p, na_p, nb_p = conv_mixer(hp, zero_a, zero_b, *wts)
            out_s, na_s, nb_s = conv_mixer(hs, state_a_conv[p], state_b_conv[p], *wts)
            a_p.append(na_p); a_s.append(na_s); b_p.append(nb_p); b_s.append(nb_s)
        else:
            wts = (w_in_rec[p], c_i_b[p], c_f_b[p], c_hn_g[p], d_hn_g[p], w_out_rec[p])
            out_p, c1, n1, m1, s1 = recurrent_mixer(hp, pos_p, zero_c, zero_n, zero_m, zero_s, *wts)
            out_s, c2, n2, m2, s2 = recurrent_mixer(hs, pos_s, state_c_C[p], state_c_n[p], state_c_m[p],
                                                    state_d_S[p], *wts)
            cc_p.append(c1); cn_p.append(n1); cm_p.append(m1); ds_p.append(s1)
            cc_s.append(c2); cn_s.append(n2); cm_s.append(m2); ds_s.append(s2)
        xp = xp + rmsnorm(out_p, norm_post[layer])
        xs = xs + rmsnorm(out_s, norm_post[layer])

    return (xp, xs,
            jnp.stack(a_p), jnp.stack(a_s), jnp.stack(b_p), jnp.stack(b_s),
            jnp.stack(cc_p), jnp.stack(cc_s), jnp.stack(cn_p), jnp.stack(cn_s),
            jnp.stack(cm_p), jnp.stack(cm_s), jnp.stack(ds_p), jnp.stack(ds_s))
```

```python
import contextlib
import re
import numpy as np
import ml_dtypes
import concourse.bass as bass
import concourse.mybir as mybir
from concourse.bass_utils import run_bass_kernel_spmd

F32 = mybir.dt.float32
BF16 = mybir.dt.bfloat16
AF = mybir.ActivationFunctionType
ALU = mybir.AluOpType

NCORES = 8
D = 1024
TP = 2048
T = 2176
NT = 17
SEQS = 16
EPS = 1e-6
TGS = [(0, 512), (512, 512), (1024, 512), (1536, 512), (2048, 128)]
NPAR = 39
BIG = 30000.0


class Prog:
    ENG = ['pe', 'act', 'dve', 'pool', 'sp']

    def __init__(self, nc):
        self.nc = nc
        self.rec = {e: [] for e in self.ENG}
        self.cnt = {e: 0 for e in self.ENG}
        self.seen = {e: {} for e in self.ENG}
        self.lastw = {}
        self.reads = {}
        self.dma_cnt = {}
        self.nbank = 0
        self.marks = []
        self.tag = ''
        self.labels = {e: [] for e in self.ENG}

    def _wait(self, eng, sname, val):
        if self.seen[eng].get(sname, 0) >= val:
            return
        self.seen[eng][sname] = val
        self.rec[eng].append(('wait', sname, val))

    def _deps(self, eng, reads, writes, pe_acc=False):
        need = {}

        def add(t):
            if t[0] == eng and eng == 'pe':
                return
            if need.get(t[0], 0) < t[1]:
                need[t[0]] = t[1]
        for k in reads:
            t = self.lastw.get(k)
            if t is not None:
                add(t)
        for k in writes:
            t = self.lastw.get(k)
            if t is not None:
                add(t)
            for s, v in self.reads.get(k, {}).items():
                if s == eng:
                    continue
                add((s, v))
        for s, v in need.items():
            self._wait(eng, s, v)

    def _mark(self, tok, reads, writes):
        for k in reads:
            d = self.reads.setdefault(k, {})
            if d.get(tok[0], 0) < tok[1]:
                d[tok[0]] = tok[1]
        for k in writes:
            self.lastw[k] = tok
            self.reads[k] = {}

    def op(self, eng, fn, reads=(), writes=()):
        self._deps(eng, reads, writes)
        self.cnt[eng] += 1
        tok = (eng, self.cnt[eng])
        self.rec[eng].append(('op', fn))
        self.labels[eng].append(self.tag)
        self._mark(tok, reads, writes)

    def mark(self, label):
        self.marks.append((label, dict(self.cnt)))

    def barrier(self):
        for e in self.ENG:
            for o in self.ENG:
                if o != e and self.cnt[o] > 0:
                    self._wait(e, o, self.cnt[o])
            for sem, v in self.dma_cnt.items():
                self._wait(e, sem, v)
        self.lastw = {}
        self.reads = {}

    def fence(self, eng, keys):
        self._deps(eng, (), keys)

    def dma(self, eng, fn, reads=(), writes=(), sem=None):
        self._deps(eng, reads, writes)
        if sem is None:
            k = writes[0] if len(writes) else reads[0]
            sem = ('dl_' if len(writes) else 'ds_') + re.sub(r'[^0-9a-zA-Z]+', '_', repr(k))
        self.dma_cnt[sem] = self.dma_cnt.get(sem, 0) + 16
        tok = (sem, self.dma_cnt[sem])
        self.rec[eng].append(('dma', fn, sem))
        self._mark(tok, reads, writes)

    def wait_all_dma(self, eng):
        for sem, v in self.dma_cnt.items():
            self._wait(eng, sem, v)

    def emit(self):
        nc = self.nc
        with contextlib.ExitStack() as st:
            sems = {}
            for e in self.ENG:
                sems[e] = st.enter_context(nc.semaphore('s_' + e))
            for s in self.dma_cnt:
                sems[s] = st.enter_context(nc.semaphore('s_' + s))
            block = st.enter_context(nc.Block())
            engs = {'pe': block.tensor, 'act': block.scalar, 'dve': block.vector,
                    'pool': block.gpsimd, 'sp': block.sync}
            for e in self.ENG:
                rec = self.rec[e]

                def body(eng, rec=rec, e=e):
                    for r in rec:
                        if r[0] == 'wait':
                            eng.wait_ge(sems[r[1]], r[2])
                        elif r[0] == 'op':
                            r[1](eng).then_inc(sems[e], 1)
                        else:
                            r[1](eng).then_inc(sems[r[2]], 16)
                engs[e](body)


class Arena:
    def __init__(self, tensor, nbytes):
        self.t = tensor
        self.n = nbytes
        self.off = 0

    def alloc(self, free_shape, dtype):
        esz = 4 if dtype == F32 else 2
        n = 1
        for s in free_shape:
            n *= s
        off = (self.off + 3) // 4 * 4
        assert off + n * esz <= self.n, ("arena overflow", off, n * esz, self.n)
        ap = self.t[:, off // 2: off // 2 + n * esz // 2]
        if dtype == F32:
            ap = ap.bitcast(F32)
        if len(free_shape) == 2:
            ap = ap.rearrange("p (a b) -> p a b", a=free_shape[0])
        elif len(free_shape) == 3:
            ap = ap.rearrange("p (a b c) -> p a b c", a=free_shape[0], b=free_shape[1])
        self.off = off + n * esz
        return ap

    def reset(self, to=0):
        self.off = to


DBG = {}
L1_STOP = [0]


class _Stop(Exception):
    pass


def build_program(layers=2, debug=False):
    nc = bass.Bass("TRN2", target_bir_lowering=False)
    DBG.clear()

    def din(name, shape, dt=F32):
        return nc.dram_tensor(name, list(shape), dt, kind="ExternalInput").ap()

    def dout(name, shape, dt=F32):
        return nc.dram_tensor(name, list(shape), dt, kind="ExternalOutput").ap()

    x_all = din("x_all", [T, D])
    cvec = din("cvec", [NPAR, D])
    npost = din("npost", [2, D])
    WA_d = din("WA", [8, 128, 8 * 4 * 128])
    WB_d = din("WB", [8, 128, 8 * 3 * 128])
    WO0_d = din("WO0", [128, 16 * D])
    st_a = din("st_a", [SEQS * 2, D])
    st_b = din("st_b", [SEQS * 30, D])
    identF_d = din("identF", [128, 128])
    identB_d = din("identB", [128, 128], BF16)

    if layers > 1:
        WQ_d = din("WQ", [8, 128, 8 * 256])
        WK_d = din("WK", [8, 128, 8 * 256])
        WVc_d = din("WVc", [4, 128, 8 * 768])
        WVd_d = din("WVd", [4, 128, 8 * 512])
        WG_d = din("WG", [128, 64])
        WO1_d = din("WO1", [128, 16 * D])
        st_C = din("st_C", [SEQS, 4, 256, 256])
        st_S = din("st_S", [SEQS, 4, 256, 256])
        st_n = din("st_n", [SEQS * 4, 256])
        st_m = din("st_m", [SEQS, 4])
        cib_d = din("cib", [4, 1])
        cfb_d = din("cfb", [4, 1])
        hng_d = din("hng", [2, D])
        cs_d = din("cs_tab", [2, 128, T])
        maskadd_d = din("maskadd", [2, 128, 128])
        rcols_d = din("rcols", [128, 24])
        rowmask_d = din("rowmask", [128, 16])
        rowsel_d = din("rowsel", [128, 16])
        scanc_d = din("scanc", [4, 2, 128])
        C_p = dout("C_p", [4, 256, 256])
        C_s = dout("C_s", [SEQS, 4, 256, 256])
        S_p = dout("S_p", [4, 256, 256])
        S_s = dout("S_s", [SEQS, 4, 256, 256])
        n_p = dout("n_p", [4, 256])
        n_s = dout("n_s", [SEQS * 4, 256])
        m_p = dout("m_p", [4, 1])
        m_s = dout("m_s", [SEQS, 4])

    y_out = dout("y_out", [T, D])
    a_p = dout("a_p", [2, D])
    a_s = dout("a_s", [SEQS * 2, D])
    b_p = dout("b_p", [30, D])
    b_s = dout("b_s", [SEQS, 30, D])

    P = Prog(nc)
    with contextlib.ExitStack() as st:
        def sb(name, shape, dt):
            return st.enter_context(nc.sbuf_tensor(name, list(shape), dt))

        hT = sb("hT", [128, 8, T], BF16)
        bigA = sb("bigA", [128, 8, T], BF16)
        bigB = sb("bigB", [128, 8, T], BF16)
        tmpF = sb("tmpF", [128, 8, 512], F32)
        xt = [sb("xt%d" % i, [128, D], F32) for i in range(2)]
        xs_bf = sb("xs_bf", [128, D], BF16)
        junk = sb("junk", [128, D], BF16)
        identF = sb("identFs", [128, 128], F32)
        identB = sb("identBs", [128, 128], BF16)
        onesB = sb("onesB", [128, 128], BF16)
        onesF = sb("onesF", [128, 128], F32)
        cm = sb("cm", [128, 8, NPAR], F32)
        gbc = sb("gbc", [128, 8, 128], F32)
        gpost = sb("gpost", [128, D], F32)
        sbt = sb("sbt", [128, 4, 128], F32)
        sat = sb("sat", [32, 128], F32)
        stg = [sb("stg%d" % i, [128, 128], F32) for i in range(2)]
        small = sb("small", [128, 16], F32)
        epsc = sb("epsc", [128, 1], F32)
        R1t = sb("R1", [128, 32000], BF16)
        R1 = Arena(R1t, 64000)
        pst = [st.enter_context(nc.psum_tensor("ps%d" % i, [128, 1024], F32)) for i in range(4)]

        def bank_ap(b):
            return pst[b // 2][:, (b % 2) * 512:(b % 2) * 512 + 512]

        pinned = set()

        def newbank():
            while (P.nbank % 8) in pinned:
                P.nbank += 1
            b = P.nbank % 8
            P.nbank += 1
            return b, bank_ap(b), ('ps', b)

        def newbank2():
            if P.nbank % 2:
                P.nbank += 1
            b = P.nbank % 8
            P.nbank += 2
            return pst[b // 2][:], [('ps', b), ('ps', b + 1)]

        def act(out, in_, func, reads, writes, **kw):
            P.op('act', lambda e: e.activation(out=out, in_=in_, func=func, **kw), reads, writes)

        def tt(eng, out, in0, in1, op, reads, writes):
            P.op(eng, lambda e: e.tensor_tensor(out=out, in0=in0, in1=in1, op=op), reads, writes)

        def ts(eng, out, in0, s1, s2, op0, op1, reads, writes):
            if op1 is None:
                P.op(eng, lambda e: e.tensor_scalar(out=out, in0=in0, scalar1=s1, scalar2=None, op0=op0), reads, writes)
            else:
                P.op(eng, lambda e: e.tensor_scalar(out=out, in0=in0, scalar1=s1, scalar2=s2, op0=op0, op1=op1), reads, writes)

        def stt(out, in0, scalar, in1, op0, op1, reads, writes):
            P.op('dve', lambda e: e.scalar_tensor_tensor(out=out, in0=in0, scalar=scalar, in1=in1, op0=op0, op1=op1), reads, writes)

        def cp(eng, out, in_, reads, writes):
            if eng == 'act':
                P.op('act', lambda e: e.copy(out=out, in_=in_), reads, writes)
            else:
                P.op(eng, lambda e: e.tensor_copy(out=out, in_=in_), reads, writes)

        def recip(out, in_, reads, writes):
            P.op('dve', lambda e: e.reciprocal(out=out, in_=in_), reads, writes)

        def memset(eng, ap, val, writes):
            P.op(eng, lambda e: e.memset(ap, val), (), writes)

        def mm(out, lhsT, rhs, start, stop, reads, writes):
            P.op('pe', lambda e: e.matmul(out, lhsT=lhsT, rhs=rhs, start=start, stop=stop), reads, writes)

        def tr(out, in_, ident, reads, writes):
            P.op('pe', lambda e: e.transpose(out, in_, ident), reads, writes)

        def load(out, in_, writes, reads=(), eng='sp', sem=None):
            P.dma(eng, lambda e: e.dma_start(out=out, in_=in_), reads=reads, writes=writes, sem=sem)

        def load_flat(dst, src, keys, row=1024):
            load(dst.rearrange("p (a b) -> p a b", b=row), src.rearrange("p (a b) -> p a b", b=row), keys, eng='pool')

        def store(out, in_, reads, sem=None):
            P.dma('sp', lambda e: e.dma_start(out=out, in_=in_), reads=reads, writes=(), sem=sem)

        def rstd_from_ss(ss, sd, rstd, kss):
            act(sd, ss, AF.Sqrt, [kss, 'epsc'], ['sd'], scale=1.0 / D, bias=epsc[:, 0:1])
            P.op('dve', lambda e: e.reciprocal(out=rstd, in_=sd), ['sd'], ['rstd'])

        def norm_transpose(xtile, kx, i):
            ss = small[:, 0:1]
            sd = small[:, 1:2]
            rstd = small[:, 2:3]
            act(junk[:], xtile, AF.Square, [kx], ['junk', 'ss'], accum_out=ss)
            rstd_from_ss(ss, sd, rstd, 'ss')
            ts('dve', xs_bf[:], xtile, rstd, None, ALU.mult, None, [kx, 'rstd'], ['xs_bf'])
            b, bap, bk = newbank()
            bb = bap.bitcast(BF16)
            for kc in range(8):
                tr(bb[:, kc * 128:(kc + 1) * 128], xs_bf[:, kc * 128:(kc + 1) * 128], identB[:], ['xs_bf', 'identB'], [bk])
            tt('dve', hT[:, :, i * 128:(i + 1) * 128], bb.rearrange("p (k t) -> p k t", k=8), gbc[:], ALU.mult,
               [bk, 'gbc'], [('hT', i)])

        def build_gbc(layer):
            for kc in range(8):
                ts('dve', gbc[:, kc, :], onesF[:], cm[:, kc, layer:layer + 1], None, ALU.mult, None,
                   ['onesF', 'cm'], ['gbc'])

        def bc_row(dram_ap, row):
            return bass.AP(dram_ap.tensor, row * D, [[0, 128], [1, D]])

        def dbg(name, ap, keys, dt=F32):
            if not debug:
                return
            shp = list(ap.shape)
            d = nc.dram_tensor("dbg_" + name, shp, dt, kind="ExternalOutput").ap()
            DBG[name] = shp
            P.dma('sp', lambda e: e.dma_start(out=d, in_=ap), reads=keys, writes=(), sem='dbg_' + name)

        load(identF[:], identF_d, ['identF'])
        load(identB[:], identB_d, ['identB'])
        load(xt[0][0:NPAR, :], cvec, [('xt', 0)])
        load(gpost[:], bc_row(npost, 0), ['gpost'])
        P.op('pool', lambda e: e.memset(onesB[:], 1.0), (), ['onesB'])
        P.op('pool', lambda e: e.memset(onesF[:], 1.0), (), ['onesF'])
        P.op('pool', lambda e: e.memset(epsc[:], EPS), (), ['epsc'])
        b, bap, bk = newbank()
        for kc in range(8):
            tr(bap[:, kc * NPAR:(kc + 1) * NPAR], xt[0][0:NPAR, kc * 128:(kc + 1) * 128], identF[0:NPAR, 0:NPAR],
               [('xt', 0), 'identF'], [bk])
        P.op('dve', lambda e, bap=bap: e.tensor_copy(out=cm[:], in_=bap[:, 0:8 * NPAR].rearrange("p (k r) -> p k r", k=8)),
             [bk], ['cm'])
        build_gbc(0)
        dbg('xt0', xt[0][0:NPAR, :], [('xt', 0)])
        dbg('identF', identF[:], ['identF'])
        P.op('act', lambda e, bap=bap: e.copy(out=tmpF[:, 0, :], in_=bap[:, 0:512]), [bk], [('tmp', 0)])
        dbg('bank', tmpF[:, 0, :], [('tmp', 0)])
        dbg('cm', cm[:], ['cm'])
        dbg('gbc', gbc[:], ['gbc'])

        P.mark('L0_0a')
        for i in range(NT):
            buf = i % 2
            load(xt[buf][:], x_all[i * 128:(i + 1) * 128, :], [('xt', buf)])
            norm_transpose(xt[buf][:], ('xt', buf), i)

        dbg('hT', hT[:, :, 0:512], [('hT', i) for i in range(4)], BF16)

        def hkeys(s0, n):
            return [('hT', i) for i in range(s0 // 128, (s0 + n) // 128)]


        P.mark('L0_B')
        R1.reset()
        Wb = [R1.alloc((8, 3, 128), BF16) for _ in range(2)]
        diag = [R1.alloc((31, 128), BF16) for _ in range(2)]
        ubf = [R1.alloc((2686,), BF16) for _ in range(2)]
        sq = R1.alloc((8, 512), BF16)
        for bf in range(2):
            P.op('pool', lambda e, bf=bf: e.memset(ubf[bf][:, 0:30], 0.0), (), [('ubh', bf)])

        def load_wb(c):
            bf = c % 2
            load_flat(Wb[bf].rearrange("p k g n -> p (k g n)"), WB_d[c], [('Wb', bf)])

        load_wb(0)
        for c in range(8):
            bf = c % 2
            if c + 1 < 8:
                load_wb(c + 1)
            ubp = ubf[bf][:, 0:2078]
            ubs = ubf[bf][:, 2078:2686].rearrange("p (s k) -> p s k", s=SEQS)
            for j in range(14, 31):
                ts('pool', diag[bf][:, j, :], identB[:], cm[:, c, 5 + j:6 + j], 1.0, ALU.mult, ALU.mult,
                   ['identB', 'cm'], [('diag', bf)])
            load(sbt[:, 0:3, :], st_b[0:384, c * 128:(c + 1) * 128].rearrange("(rt p) n -> p rt n", p=128), ['sbt0'])
            load(sbt[0:96, 3, :], st_b[384:480, c * 128:(c + 1) * 128], ['sbt1'])
            b, bap, bk = newbank()
            for rt in range(4):
                nr = 128 if rt < 3 else 96
                tr(bap[:, rt * 128:rt * 128 + nr], sbt[0:nr, rt, :], identF[0:nr, 0:nr],
                   ['sbt0', 'sbt1', 'identF'], [bk])
            P.op('act', lambda e, bap=bap, ubs=ubs: e.copy(out=ubs[:, :, 0:30],
                                                            in_=bap[:, 0:480].rearrange("p (s k) -> p s k", s=SEQS)),
                 [bk], [('ubsh', bf)])
            for gi, (s0, n) in enumerate(TGS):
                samp = (gi == 4)
                banks = [newbank() for _ in range(3)]
                for g in range(3):
                    for kc in range(8):
                        mm(banks[g][1][:, 0:n], Wb[bf][:, kc, g, :], hT[:, kc, s0:s0 + n], kc == 0, kc == 7,
                           [('Wb', bf)] + hkeys(s0, n), [banks[g][2]])
                tb = gi % 2
                sg = tmpF[:, tb, 0:n]
                ub32 = tmpF[:, 2 + tb, 0:n]
                act(sg, banks[1][1][:, 0:n], AF.Tanh, [banks[1][2]], [('tmp', tb)], scale=0.5)
                stt(ub32, sg, 1.0, banks[0][1][:, 0:n], ALU.add, ALU.mult, [banks[0][2], ('tmp', tb)], [('tmp', 2 + tb)])
                if not samp:
                    P.op('pool', lambda e, ubp=ubp, ub32=ub32, s0=s0, n=n: e.tensor_scalar(out=ubp[:, 30 + s0:30 + s0 + n], in0=ub32, scalar1=0.5, scalar2=1.0, op0=ALU.mult, op1=ALU.mult),
                         [('tmp', 2 + tb)], [('ub', bf, gi)])
                else:
                    P.op('pool', lambda e, ubs=ubs, ub32=ub32: e.tensor_scalar(out=ubs[:, :, 30:38],
                                                                              in0=ub32.rearrange("p (s k) -> p s k", s=SEQS),
                                                                              scalar1=0.5, scalar2=1.0, op0=ALU.mult, op1=ALU.mult),
                         [('tmp', 2 + tb)], [('ub', bf, gi)])
                act(bigB[:, c, s0:s0 + n], banks[2][1][:, 0:n], AF.Silu, [banks[2][2]], [('bigB', c, gi)])
                KD = 14
                b, bap, bk = newbank()
                acc = tmpF[:, 4 + tb, 0:n]
                if not samp:
                    rk = [('ub', bf, gi), ('ubh', bf)] + ([('ub', bf, gi - 1)] if gi > 0 else [])
                    tapv = lambda j: ubp[:, s0 + j:s0 + j + n]
                    accv = acc
                    outp = bap[:, 0:n]
                else:
                    rk = [('ub', bf, gi), ('ubsh', bf)]
                    tapv = lambda j: ubs[:, :, j:j + 8]
                    accv = acc.rearrange("p (s k) -> p s k", s=SEQS)
                    outp = bap[:, 0:128].rearrange("p (s k) -> p s k", s=SEQS)
                for j in range(KD, 31):
                    mm(outp, diag[bf][:, j, :], tapv(j), j == KD, j == 30, rk + [('diag', bf)], [bk])
                ts('dve', accv, tapv(0), cm[:, c, 5:6], None, ALU.mult, None, rk + ['cm'], [('tmp', 4 + tb)])
                for j in range(1, KD):
                    stt(accv, tapv(j), cm[:, c, 5 + j:6 + j], accv, ALU.mult, ALU.add, rk + ['cm', ('tmp', 4 + tb)], [('tmp', 4 + tb)])
                stt(bigA[:, c, s0:s0 + n], bap[:, 0:n], cm[:, c, 36:37], acc, ALU.add, ALU.add,
                    [bk, 'cm', ('tmp', 4 + tb)], [('bigA', c, gi)])
                if gi == 3:
                    b2, bap2, bk2 = newbank()
                    tr(bap2[0:30, 0:128], ub32[:, 482:512], identF[:], [('tmp', 2 + tb), 'identF'], [bk2])
                    sgi = c % 2
                    P.op('act', lambda e, bap2=bap2, sgi=sgi: e.activation(out=stg[sgi][0:30, :], in_=bap2[0:30, 0:128], func=AF.Copy, scale=0.5), [bk2], [('stg', sgi)])
                    store(b_p[:, c * 128:(c + 1) * 128], stg[sgi][0:30, :], [('stg', sgi)])
                if samp:
                    b2, bap2, bk2 = newbank()
                    tr(bap2[:, 0:128], ub32[:, 0:128], identF[:], [('tmp', 2 + tb), 'identF'], [bk2])
                    sgi = c % 2
                    P.op('act', lambda e, bap2=bap2, sgi=sgi: e.activation(out=stg[sgi][:], in_=bap2[:, 0:128], func=AF.Copy, scale=0.5), [bk2], [('stg', sgi)])
                    for s in range(SEQS):
                        store(b_s[s, 22:30, c * 128:(c + 1) * 128], stg[sgi][s * 8:(s + 1) * 8, :], [('stg', sgi)])
        P.dma('sp', lambda e: e.dma_start(out=b_s[:, 0:22, :], in_=st_b.rearrange("(s k) n -> s k n", s=SEQS)[:, 8:30, :]),
              reads=(), writes=(), sem='d_d2d')

        P.mark('L0_LN')
        allA = lambda gi: [('bigA', c, gi) for c in range(8)]
        for gi, (s0, n) in enumerate(TGS):
            b1, s1ap, s1k = newbank()
            for c in range(8):
                mm(s1ap[:, 0:n], onesB[:], bigA[:, c, s0:s0 + n], c == 0, c == 7, ['onesB', ('bigA', c, gi)], [s1k])
            P.op('pool', lambda e, s0=s0, n=n: e.tensor_tensor(out=sq[:, :, 0:n], in0=bigA[:, :, s0:s0 + n],
                                                              in1=bigA[:, :, s0:s0 + n], op=ALU.mult),
                 allA(gi), ['sq'])
            b2, s2ap, s2k = newbank()
            for c in range(8):
                mm(s2ap[:, 0:n], onesB[:], sq[:, c, 0:n], c == 0, c == 7, ['onesB', 'sq'], [s2k])
            mean = tmpF[:, 4, 0:n]
            var = tmpF[:, 5, 0:n]
            act(mean, s1ap[:, 0:n], AF.Copy, [s1k], [('tmp', 4)], scale=1.0 / D)
            tt('dve', var, mean, mean, ALU.mult, [('tmp', 4)], [('tmp', 5)])
            stt(var, s2ap[:, 0:n], 1.0 / D, var, ALU.mult, ALU.subtract, [s2k, ('tmp', 5)], [('tmp', 5)])
            act(var, var, AF.Sqrt, [('tmp', 5), 'epsc'], [('tmp', 5)], bias=epsc[:, 0:1], scale=1.0)
            P.op('dve', lambda e, var=var: e.reciprocal(out=var, in_=var), [('tmp', 5)], [('tmp', 5)])
            for c in range(8):
                tb = c % 2
                t1 = tmpF[:, 6 + tb, 0:n]
                u = tmpF[:, tb, 0:n]
                tt('dve', t1, bigA[:, c, s0:s0 + n], mean, ALU.subtract, [('bigA', c, gi), ('tmp', 4)], [('tmp', 6 + tb)])
                tt('dve', t1, t1, var, ALU.mult, [('tmp', 6 + tb), ('tmp', 5)], [('tmp', 6 + tb)])
                act(u, t1, AF.Silu, [('tmp', 6 + tb), 'cm'], [('tmp', tb)], scale=cm[:, c, 37:38], bias=cm[:, c, 38:39])
                tt('pool', bigB[:, c, s0:s0 + n], u, bigB[:, c, s0:s0 + n], ALU.mult,
                   [('tmp', tb), ('bigB', c, gi)], [('bigB', c, gi)])

        P.mark('L0_A')
        R1.reset()
        Wa = [R1.alloc((8, 4, 128), BF16) for _ in range(2)]
        uaf = R1.alloc((2210,), F32)
        wout = R1.alloc((16, D), BF16)
        oldkeys = ['sq'] + [('Wb', b_) for b_ in range(2)] + [('diag', b_) for b_ in range(2)] \
            + [('ub', b_, g_) for b_ in range(2) for g_ in range(5)] + [('ubh', b_) for b_ in range(2)] + [('ubsh', b_) for b_ in range(2)]
        P.fence('pool', oldkeys)
        P.fence('dve', oldkeys)
        P.fence('act', oldkeys)
        P.op('pool', lambda e: e.memset(uaf[:, 0:2], 0.0), [], ['uah'])

        def load_wa(c):
            bf = c % 2
            load_flat(Wa[bf].rearrange("p k g n -> p (k g n)"), WA_d[c], [('Wa', bf)])

        load_wa(0)
        uap = uaf[:, 0:2050]
        uas = uaf[:, 2050:2210].rearrange("p (s k) -> p s k", s=SEQS)
        for c in range(8):
            bf = c % 2
            if c + 1 < 8:
                load_wa(c + 1)
            if c == 1:
                for q in range(4):
                    load_flat(wout[:, q * 4:(q + 1) * 4, :].rearrange("p k n -> p (k n)"), WO0_d[:, q * 4096:(q + 1) * 4096], [('wout', q)])
            load(sat[:], st_a[:, c * 128:(c + 1) * 128], ['sat'])
            b, bap, bk = newbank()
            tr(bap[:, 0:32], sat[0:32, :], identF[0:32, 0:32], ['sat', 'identF'], [bk])
            P.op('act', lambda e, bap=bap: e.copy(out=uas[:, :, 0:2], in_=bap[:, 0:32].rearrange("p (s k) -> p s k", s=SEQS)),
                 [bk], ['uash'])
            for gi, (s0, n) in enumerate(TGS):
                samp = (gi == 4)
                banks = [newbank() for _ in range(4)]
                for g in range(4):
                    for kc in range(8):
                        mm(banks[g][1][:, 0:n], Wa[bf][:, kc, g, :], hT[:, kc, s0:s0 + n], kc == 0, kc == 7,
                           [('Wa', bf)] + hkeys(s0, n), [banks[g][2]])
                tb = gi % 2
                t1 = tmpF[:, tb, 0:n]
                sz = tmpF[:, 2 + tb, 0:n]
                acc = tmpF[:, 4 + tb, 0:n]
                t2 = tmpF[:, 6 + tb, 0:n]
                act(t1, banks[2][1][:, 0:n], AF.Copy, [banks[2][2]], [('tmp', tb)])
                if not samp:
                    uw = uap[:, 2 + s0:2 + s0 + n]
                    taps = [uap[:, s0 + j:s0 + j + n] for j in range(3)]
                    accv = acc
                    t1v = t1
                    acv = banks[1][1][:, 0:n]
                else:
                    uw = uas[:, :, 2:10]
                    taps = [uas[:, :, j:j + 8] for j in range(3)]
                    accv = acc.rearrange("p (s k) -> p s k", s=SEQS)
                    t1v = t1.rearrange("p (s k) -> p s k", s=SEQS)
                    acv = banks[1][1][:, 0:n].rearrange("p (s k) -> p s k", s=SEQS)
                tt('dve', uw, acv, t1v, ALU.mult, [banks[1][2], ('tmp', tb)], [('ua', gi)])
                rk = [('ua', gi), 'uah', 'uash'] + ([('ua', gi - 1)] if gi > 0 else [])
                ts('dve', accv, taps[0], cm[:, c, 2:3], None, ALU.mult, None, rk + ['cm'], [('tmp', 4 + tb)])
                stt(accv, taps[1], cm[:, c, 3:4], accv, ALU.mult, ALU.add, rk + [('tmp', 4 + tb), 'cm'], [('tmp', 4 + tb)])
                stt(accv, taps[2], cm[:, c, 4:5], accv, ALU.mult, ALU.add, rk + [('tmp', 4 + tb), 'cm'], [('tmp', 4 + tb)])
                act(sz, banks[3][1][:, 0:n], AF.Silu, [banks[3][2]], [('tmp', 2 + tb)])
                tt('dve', t2, banks[0][1][:, 0:n], sz, ALU.mult, [banks[0][2], ('tmp', 2 + tb)], [('tmp', 6 + tb)])
                tt('pool', bigA[:, c, s0:s0 + n], t2, acc, ALU.mult, [('tmp', 6 + tb), ('tmp', 4 + tb)], [('bigA', c, gi)])
                if gi == 3:
                    b2, bap2, bk2 = newbank()
                    tr(bap2[0:2, 0:128], uap[:, 2048:2050], identF[:], [('ua', 3), 'identF'], [bk2])
                    sgi = c % 2
                    P.op('act', lambda e, bap2=bap2, sgi=sgi: e.copy(out=stg[sgi][0:2, :], in_=bap2[0:2, 0:128]), [bk2], [('stg', sgi)])
                    store(a_p[:, c * 128:(c + 1) * 128], stg[sgi][0:2, :], [('stg', sgi)])
                if samp:
                    P.op('dve', lambda e, t1=t1: e.tensor_copy(out=t1[:, 0:32].rearrange("p (s k) -> p s k", s=SEQS), in_=uas[:, :, 8:10]),
                         [('ua', 4)], [('tmp', tb)])
                    b2, bap2, bk2 = newbank()
                    tr(bap2[0:32, 0:128], t1[:, 0:32], identF[:], [('tmp', tb), 'identF'], [bk2])
                    sgi = c % 2
                    P.op('act', lambda e, bap2=bap2, sgi=sgi: e.copy(out=stg[sgi][0:32, :], in_=bap2[0:32, 0:128]), [bk2], [('stg', sgi)])
                    store(a_s[:, c * 128:(c + 1) * 128], stg[sgi][0:32, :], [('stg', sgi)])

        def out_phase(layer, wout, src_dram, do_norm):
            xsb = [R1.alloc((D,), BF16) for _ in range(2)]
            o1b = [R1.alloc((D,), F32) for _ in range(2)]
            jk = R1.alloc((D,), BF16)
            obank = {}

            def X(i):
                buf = i % 2
                load(xt[buf][:], src_dram[i * 128:(i + 1) * 128, :], [('xt', buf)], reads=[('yrow', i)])
                oap, oks = newbank2()
                for half in range(2):
                    for kc in range(16):
                        src = bigA if kc < 8 else bigB
                        mm(oap[:, half * 512:(half + 1) * 512], src[:, kc % 8, i * 128:(i + 1) * 128],
                           wout[:, kc, half * 512:(half + 1) * 512], kc == 0, kc == 15,
                           [('y', kc, i), ('wout', kc // 4)], [oks[half]])
                obank[i] = (oap, oks)

            def Y(i):
                buf = i % 2
                oap, oks = obank.pop(i)
                ss = small[:, 4 + 4 * buf:5 + 4 * buf]
                sd = small[:, 5 + 4 * buf:6 + 4 * buf]
                rstd = small[:, 6 + 4 * buf:7 + 4 * buf]
                kq = ('sm', buf)
                act(jk, oap, AF.Square, oks, ['jk', kq], accum_out=ss)
                act(sd, ss, AF.Sqrt, [kq, 'epsc'], [kq], scale=1.0 / D, bias=epsc[:, 0:1])
                recip(rstd, sd, [kq], [kq])
                stt(o1b[buf], oap, rstd, gpost[:], ALU.mult, ALU.mult, oks + [kq, 'gpost'], [('o1', buf)])
                tt('pool', xt[buf][:], o1b[buf], xt[buf][:], ALU.add, [('o1', buf), ('xt', buf)], [('xt', buf)])
                P.dma('sp', lambda e, i=i, buf=buf: e.dma_start(out=y_out[i * 128:(i + 1) * 128, :], in_=xt[buf][:]),
                      reads=[('xt', buf)], writes=[('yrow', i)], sem='ds_xt%d' % buf)
                if do_norm:
                    ss2 = small[:, 12 + buf:13 + buf]
                    sd2 = small[:, 14 + buf:15 + buf]
                    act(jk, xt[buf][:], AF.Square, [('xt', buf)], ['jk', ('sm2', buf)], accum_out=ss2)
                    act(sd2, ss2, AF.Sqrt, [('sm2', buf), 'epsc'], [('sm2', buf)], scale=1.0 / D, bias=epsc[:, 0:1])
                    recip(sd2, sd2, [('sm2', buf)], [('sm2', buf)])
                    ts('dve', xsb[buf], xt[buf][:], sd2, None, ALU.mult, None, [('xt', buf), ('sm2', buf)], [('xsb', buf)])

            def Z(i):
                buf = i % 2
                b, bap, bk = newbank()
                bb = bap.bitcast(BF16)
                for kc in range(8):
                    tr(bb[:, kc * 128:(kc + 1) * 128], xsb[buf][:, kc * 128:(kc + 1) * 128], identB[:], [('xsb', buf), 'identB'], [bk])
                tt('dve', hT[:, :, i * 128:(i + 1) * 128], bb.rearrange("p (k t) -> p k t", k=8), gbc[:], ALU.mult,
                   [bk, 'gbc'], [('hT', i)])

            X(0)
            X(1)
            for i in range(NT):
                Y(i)
                if i + 2 < NT:
                    X(i + 2)
                if do_norm:
                    Z(i)

        if layers > 1:
            build_gbc(1)
        P.barrier()
        P.mark('L0_out')
        R1.reset(0)
        out_phase(0, wout, x_all, layers > 1)

        if layers > 1:
          try:
            P.barrier()
            P.mark('L1_gates')
            GAM = [1.0 - 2.0 ** (-5.0 - hh) for hh in range(4)]
            RA = Arena(bigA[:].rearrange("p k t -> p (k t)"), 8 * T * 2)
            RB = Arena(bigB[:].rearrange("p k t -> p (k t)"), 8 * T * 2)
            rows = [RA.alloc((T,), F32) for _ in range(4)] + [RB.alloc((T,), F32) for _ in range(4)]
            r = [x_[0:4, :] for x_ in rows]
            R1.reset()
            wq = R1.alloc((8, 256), BF16)
            wk = R1.alloc((8, 256), BF16)
            wvoz = R1.alloc((8, 768), BF16)
            qT = R1.alloc((2, T), BF16)
            kT = R1.alloc((2, T), BF16)
            Spr = R1.alloc((2, 257), F32)
            Sbp = R1.alloc((2, 258), BF16)
            Sld = [R1.alloc((2, 256), F32), junk[:].bitcast(F32).rearrange("p (a b) -> p a b", a=2),
                   xs_bf[:].bitcast(F32).rearrange("p (a b) -> p a b", a=2)]
            Sbs = [R1.alloc((2, 258), BF16) for _ in range(3)]
            gcm1 = R1.alloc((16,), F32)
            nT = R1.alloc((2, 64), F32)
            nnewT = R1.alloc((2, 64), F32)
            cols = R1.alloc((NT, 16), F32)
            sclbc = R1.alloc((4, 33), F32)
            xsel = R1.alloc((4, 33), F32)
            v_ext = [R1.alloc((258,), BF16) for _ in range(3)]
            PT = [R1.alloc((128,), BF16) for _ in range(3)]
            kw = [R1.alloc((256,), BF16) for _ in range(3)]
            y_tok = [R1.alloc((256,), BF16) for _ in range(2)]
            qblk = R1.alloc((2, 16, 128), BF16)
            kwblk = R1.alloc((2, 128), BF16)
            rcols = R1.alloc((24,), F32)
            mhalf = R1.alloc((2,), F32)
            rowmask = R1.alloc((16,), F32)
            rowsel = R1.alloc((16,), F32)
            wg = R1.alloc((8, 8), BF16)
            scanc = sbt[:, 2:4, :]
            seed_s = sat
            cibt = R1.alloc((4,), F32)
            cs = gbc[:].rearrange("p k t -> p (k t)").rearrange("p (a b) -> p a b", a=2)

            def load_qk(mx, h):
                u = mx * 4 + h
                load_flat(wq.rearrange("p k n -> p (k n)"), WQ_d[u], ['wq'])
                load_flat(wk.rearrange("p k n -> p (k n)"), WK_d[u], ['wk'])

            def load_voz(mx, h):
                if mx == 0:
                    load_flat(wvoz.rearrange("p k n -> p (k n)"), WVc_d[h], [('wvoz', 0)])
                else:
                    load(wvoz[:, :, 0:512], WVd_d[h].rearrange("p (k n) -> p k n", k=8), [('wvoz', 0)], eng='pool')

            load(wg.rearrange("p k n -> p (k n)"), WG_d, ['wg'], eng='pool')
            load_qk(0, 0)
            load_voz(0, 0)
            load(gpost[:], bc_row(npost, 1), ['gpost'])
            load(xt[0][0:2, :], hng_d, [('xt', 0)])
            b, bap, bk = newbank()
            for kc in range(8):
                tr(bap[:, kc * 2:kc * 2 + 2], xt[0][0:2, kc * 128:(kc + 1) * 128], identF[0:2, 0:2], [('xt', 0), 'identF'], [bk])
            act(gcm1[:].rearrange("p (m k) -> p m k", m=2), bap[:, 0:16].rearrange("p (k m) -> p m k", m=2), AF.Copy, [bk], ['gcm1'], scale=0.5)
            memset('pool', mhalf[:], -0.5, ['mhalf'])
            load(sbt[:, 0, :], maskadd_d[0], ['maskp'])
            load(sbt[:, 1, :], maskadd_d[1], ['masks'])
            load(rcols, rcols_d, ['rcols'])
            load(rowmask, rowmask_d, ['rowmask'])
            load(rowsel, rowsel_d, ['rowsel'])
            load(scanc[0:4], scanc_d, ['scanc'])
            load(cibt[0:4, 0:1], cib_d, ['cib0'])
            load(cibt[0:4, 1:2], cfb_d, ['cib1'])
            ts('dve', cibt[0:4, 1:2], cibt[0:4, 1:2], -1.0, None, ALU.mult, None, ['cib1'], ['cib1'])
            memset('pool', cibt[0:4, 2:3], 1.0, ['cib2'])
            memset('pool', cibt[0:4, 3:4], 0.0, ['cib3'])
            memset('pool', seed_s[0:4, :], -BIG, ['seed'])
            P.dma('sp', lambda e: e.dma_start(out=seed_s[0:4, 0:128:8], in_=st_m.rearrange("s h -> h s"),
                                              allow_slow_non_contiguous=True), reads=(), writes=['seed'])
            memset('pool', qblk[:], 0.0, ['qblk'])
            memset('pool', xsel[:], 0.0, ['xsel'])
            for bf in range(3):
                memset('pool', v_ext[bf][:, 256:258], 1.0, [('vone', bf)])
            load(stg[0][0:64, :], st_n[:, 0:128], [('stg', 0)])
            load(stg[1][0:64, :], st_n[:, 128:256], [('stg', 1)])
            b, bap, bk = newbank()
            for c in range(2):
                tr(bap[:, c * 64:(c + 1) * 64], stg[c][0:64, :], identF[0:64, 0:64], [('stg', c), 'identF'], [bk])
            cp('act', nT, bap[:, 0:128].rearrange("p (c j) -> p c j", c=2), [bk], ['nT'])

            for gi, (s0, n) in enumerate(TGS):
                bi = newbank()
                bfk = newbank()
                for kc in range(8):
                    mm(bi[1][0:4, 0:n], wg[:, kc, 0:4], hT[:, kc, s0:s0 + n], kc == 0, kc == 7, ['wg'] + hkeys(s0, n), [bi[2]])
                for kc in range(8):
                    mm(bfk[1][0:4, 0:n], wg[:, kc, 4:8], hT[:, kc, s0:s0 + n], kc == 0, kc == 7, ['wg'] + hkeys(s0, n), [bfk[2]])
                act(r[0][:, s0:s0 + n], bi[1][0:4, 0:n], AF.Identity, [bi[2], 'cib0'], [('r', 0)], bias=cibt[0:4, 0:1], scale=1.0)
                act(r[1][:, s0:s0 + n], bfk[1][0:4, 0:n], AF.Exp, [bfk[2], 'cib1'], [('r', 1)], bias=cibt[0:4, 1:2], scale=-1.0)
            act(r[1], r[1], AF.Ln, [('r', 1), 'cib2'], [('r', 1)], bias=cibt[0:4, 2:3], scale=1.0)
            ts('dve', r[1], r[1], -1.0, None, ALU.mult, None, [('r', 1)], [('r', 1)])

            def scan(out, d0, d1, init, op0, op1, reads, writes):
                P.op('dve', lambda e: e.tensor_tensor_scan(out=out, data0=d0, data1=d1, initial=init, op0=op0, op1=op1), reads, writes)

            scan(r[2][:, 0:TP], cibt[0:4, 2:3].to_broadcast([4, TP]), r[1][:, 0:TP], 0.0, ALU.mult, ALU.add,
                 [('r', 1), 'cib2'], [('r', 2)])
            scan(r[2][:, TP:T], scanc[0:4, 0, :], r[1][:, TP:T], 0.0, ALU.mult, ALU.add, [('r', 1), 'scanc'], [('r', 2)])
            tt('dve', r[0], r[0], r[2], ALU.subtract, [('r', 0), ('r', 2)], [('r', 0)])
            scan(r[3][:, 0:TP], cibt[0:4, 3:4].to_broadcast([4, TP]), r[0][:, 0:TP], 0.0, ALU.add, ALU.max,
                 [('r', 0), 'cib3'], [('r', 3)])
            tt('dve', r[1][:, TP:T], r[0][:, TP:T], seed_s[0:4, :], ALU.max, [('r', 0), ('r', 1), 'seed'], [('r', 1)])
            scan(r[3][:, TP:T], scanc[0:4, 1, :], r[1][:, TP:T], 0.0, ALU.add, ALU.max, [('r', 1), 'scanc'], [('r', 3)])
            memset('dve', r[1][:, 0:128], 0.0, [('r', 1)])
            cp('dve', r[1][:, 128:TP].rearrange("p (j t) -> p j t", j=15),
               r[3][:, 127:1920:128].unsqueeze(2).to_broadcast([4, 15, 128]), [('r', 3)], [('r', 1)])
            cp('dve', r[1][:, TP:T].rearrange("p (j t) -> p j t", j=16),
               seed_s[0:4, 0:128:8].unsqueeze(2).to_broadcast([4, 16, 8]), ['seed'], [('r', 1)])
            cp('dve', r[4][:, 0:TP].rearrange("p (j t) -> p j t", j=16),
               r[3][:, 127:TP:128].unsqueeze(2).to_broadcast([4, 16, 128]), [('r', 3)], [('r', 4)])
            cp('dve', r[4][:, TP:T].rearrange("p (j t) -> p j t", j=16),
               r[3][:, TP + 7:T:8].unsqueeze(2).to_broadcast([4, 16, 8]), [('r', 3)], [('r', 4)])
            tt('dve', r[5], r[0], r[1], ALU.subtract, [('r', 0), ('r', 1)], [('r', 5)])
            act(r[5], r[5], AF.Exp, [('r', 5)], [('r', 5)])
            tt('dve', r[6], r[2], r[1], ALU.add, [('r', 2), ('r', 1)], [('r', 6)])
            act(r[6], r[6], AF.Exp, [('r', 6)], [('r', 6)], scale=-1.0)
            tt('dve', r[2], r[2], r[3], ALU.add, [('r', 2), ('r', 3)], [('r', 2)])
            P.dma('sp', lambda e: e.dma_start(out=m_p, in_=r[2][:, TP - 1:TP]), reads=[('r', 2)], writes=(), sem='ds_mp')
            P.dma('sp', lambda e: e.dma_start(out=m_s.rearrange("s h -> h s"), in_=r[2][:, TP + 7:T:8],
                                              allow_slow_non_contiguous=True), reads=[('r', 2)], writes=(), sem='ds_ms')
            tt('dve', r[7], r[0], r[4], ALU.subtract, [('r', 0), ('r', 4)], [('r', 7)])
            act(r[7], r[7], AF.Exp, [('r', 7)], [('r', 7)])
            tt('dve', r[1], r[1], r[4], ALU.subtract, [('r', 1), ('r', 4)], [('r', 1)])
            act(r[1], r[1], AF.Exp, [('r', 1)], [('r', 1)])
            for i in range(NT):
                b, bap, bk = newbank()
                for q_, ri in enumerate([5, 6, 7, 1]):
                    tr(bap[:, q_ * 4:(q_ + 1) * 4], r[ri][:, i * 128:(i + 1) * 128], identF[0:4, 0:4], [('r', ri), 'identF'], [bk])
                cp('act', cols[:, i, :], bap[:, 0:16], [bk], ['cols'])
            ts('dve', xsel[:, :, 0:16], cols[:, 0:16, 12:16].rearrange("p i h -> p h i"), rowsel[:, 0:1], None, ALU.mult, None,
               ['cols', 'rowsel', 'xsel'], ['xsel'])
            for hh in range(4):
                ts('dve', xsel[:, hh, 16:32], rowsel[:, 0:16], cols[:, 16, 12 + hh:13 + hh], None, ALU.mult, None,
                   ['cols', 'rowsel', 'xsel'], ['xsel'])
            b, bap, bk = newbank()
            mm(bap[:, 0:132], onesF[:], xsel[:].rearrange("p h j -> p (h j)"), True, True, ['onesF', 'xsel'], [bk])
            cp('act', sclbc[:].rearrange("p h j -> p (h j)"), bap[:, 0:132], [bk], ['sclbc'])
            if L1_STOP[0] == 1:
                raise _Stop()

            def tslots(i):
                if i == 16:
                    return tmpF[:, 6, :], tmpF[:, 7, :], 'S', 2
                return tmpF[:, 2 * (i % 3), :], tmpF[:, 2 * (i % 3) + 1, :], i % 3, i % 2

            nP = R1.alloc((8,), F32)
            units = [(mx, h) for mx in range(2) for h in range(4)]
            for ui, (mx, h) in enumerate(units):
                ybig = bigA if mx == 0 else bigB
                P.mark('U%d_qk' % ui)
                gam = GAM[h]
                for gi, (s0, n) in enumerate(TGS):
                    if mx == 1:
                        load(cs[:, 0, 0:n], cs_d[0, :, s0:s0 + n], ['cs0'])
                        load(cs[:, 1, 0:n], cs_d[1, :, s0:s0 + n], ['cs1'])
                    for which, (wt, wkey, dst, sc) in enumerate([(wq, 'wq', qT, 1.0), (wk, 'wk', kT, 1.0 / 16.0)]):
                        bks = [newbank() for _ in range(2)]
                        for c in range(2):
                            for kc in range(8):
                                mm(bks[c][1][:, 0:n], wt[:, kc, c * 128:(c + 1) * 128], hT[:, kc, s0:s0 + n], kc == 0, kc == 7,
                                   [wkey] + hkeys(s0, n), [bks[c][2]])
                        dk = ('qk', which, gi)
                        if mx == 0:
                            for c in range(2):
                                act(dst[:, c, s0:s0 + n], bks[c][1][:, 0:n], AF.Copy, [bks[c][2]], [dk], scale=sc)
                        else:
                            x1 = bks[0][1][:, 0:n]
                            x2 = bks[1][1][:, 0:n]
                            ta = tmpF[:, 6, 0:n]
                            tb_ = tmpF[:, 7, 0:n]
                            stt(ta, x1, sc, cs[:, 0, 0:n], ALU.mult, ALU.mult, [bks[0][2], 'cs0'], [('tmp', 6)])
                            stt(tb_, x2, sc, cs[:, 1, 0:n], ALU.mult, ALU.mult, [bks[1][2], 'cs1'], [('tmp', 7)])
                            tt('pool', dst[:, 0, s0:s0 + n], ta, tb_, ALU.subtract, [('tmp', 6), ('tmp', 7)], [dk])
                            stt(ta, x2, sc, cs[:, 0, 0:n], ALU.mult, ALU.mult, [bks[1][2], 'cs0'], [('tmp', 6)])
                            stt(tb_, x1, sc, cs[:, 1, 0:n], ALU.mult, ALU.mult, [bks[0][2], 'cs1'], [('tmp', 7)])
                            tt('pool', dst[:, 1, s0:s0 + n], ta, tb_, ALU.add, [('tmp', 6), ('tmp', 7)], [dk])
                if L1_STOP[0] == 2 or L1_STOP[0] == 100 * ui + 2:
                    raise _Stop()
                if ui + 1 < len(units):
                    load_qk(*units[ui + 1])
                memset('pool', Spr[:], 0.0, ['Spr'])
                memset('pool', Sbp[:], 0.0, ['Sbp'])

                P.mark('U%d_tiles' % ui)

                def qkkeys(i):
                    g_ = min(i // 4, 4)
                    return [('qk', 0, g_), ('qk', 1, g_)]

                st_dram = st_C if mx == 0 else st_S
                out_dram = C_s if mx == 0 else S_s
                samp_bank = {}

                def sA(i):
                    P.tag = 'sA(' + str(i) + ')'
                    samp = (i == 16)
                    tsl = slice(i * 128, (i + 1) * 128)
                    S0, S2_, kid, bi = tslots(i)
                    ksig, ksz = (('tmp', 6), ('tmp', 6)) if samp else (('sig', kid), ('sz', kid))
                    b0 = newbank()
                    for kc in range(8):
                        mm(b0[1][:, 0:512], hT[:, kc, tsl], wvoz[:, kc, 0:512], kc == 0, kc == 7,
                           [('hT', i), ('wvoz', 0)], [b0[2]])
                    if mx == 0:
                        b1 = newbank()
                        for kc in range(8):
                            mm(b1[1][:, 0:256], hT[:, kc, tsl], wvoz[:, kc, 512:768], kc == 0, kc == 7,
                               [('hT', i), ('wvoz', 0)], [b1[2]])
                        vsrc, vkey = b1[1][:, 0:256], b1[2]
                        zsrc = b0[1][:, 256:512]
                    else:
                        vsrc, vkey = b0[1][:, 256:512], b0[2]
                        zsrc = b0[1][:, 0:256]
                    bs_ = newbank()
                    for c in range(2):
                        mm(bs_[1][:, 0:128], kT[:, c, tsl], qT[:, c, tsl], c == 0, c == 1, qkkeys(i), [bs_[2]])
                    bkt = newbank()
                    bkb = bkt[1].bitcast(BF16)
                    for c in range(2):
                        tr(bkb[:, c * 128:(c + 1) * 128], kT[:, c, tsl], identB[:], qkkeys(i) + ['identB'], [bkt[2]])
                    if mx == 0:
                        act(S0[:, 0:512], b0[1][:, 0:512], AF.Tanh, [b0[2]], [ksig, ksz], scale=0.5)
                        ts('pool', S0[:, 0:256], S0[:, 0:256], 0.5, 0.5, ALU.mult, ALU.add, [ksig], [ksig])
                        ea = cols[:, i, h:h + 1]
                        wcol = cols[:, i, 8 + h:9 + h]
                    else:
                        act(S0[:, 256:512], zsrc, AF.Tanh, [b0[2]], [ksz], scale=0.5)
                        ea = rcols[:, (4 if samp else 0) + h:(4 if samp else 0) + h + 1]
                        wcol = rcols[:, (12 if samp else 8) + h:(12 if samp else 8) + h + 1]
                    act(v_ext[bi][:, 0:256], vsrc, AF.Copy, [vkey], [('v', bi)])
                    stt(S0[:, 256:512], S0[:, 256:512], 1.0, zsrc, ALU.add, ALU.mult, [ksz, b0[2]], [ksz])
                    stt(PT[bi], bs_[1][:, 0:128], ea, sbt[:, 1 if samp else 0, :], ALU.mult, ALU.mult,
                        [bs_[2], 'cols', 'rcols', 'masks' if samp else 'maskp'], [('PT', bi)])
                    act(kw[bi], bkb[:, 0:256], AF.Copy, [bkt[2], 'cols', 'rcols'], [('kw', bi)], scale=wcol)

                def readout_tail(i, bN):
                    samp = (i == 16)
                    S0, S2_, kid, bi = tslots(i)
                    ksig = ('tmp', 6) if samp else ('sig', kid)
                    kN = ('tmp', 7) if samp else ('N', kid)
                    st6 = S2_[:, 258:264]
                    mv = S2_[:, 264:266]
                    dcol = S2_[:, 268:269]
                    rcol = S2_[:, 269:270]
                    hc = S2_[:, 0:256]
                    if mx == 0:
                        ts('dve', dcol, bN[1][:, 256:257], -1.0, bN[1][:, 256:257], ALU.mult, ALU.max, [bN[2]], [kN])
                        ts('dve', dcol, dcol, cols[:, i, 4 + h:5 + h], None, ALU.max, None, [kN, 'cols'], [kN])
                        recip(rcol, dcol, [kN], [kN])
                        stt(hc, bN[1][:, 0:256], rcol, S0[:, 0:256], ALU.mult, ALU.mult, [bN[2], kN, ksig], [kN])
                    else:
                        act(hc, bN[1][:, 0:256], AF.Copy, [bN[2]], [kN])
                    P.op('dve', lambda e, st6=st6, hc=hc: e.bn_stats(out=st6, in_=hc), [kN], [kN])
                    P.op('dve', lambda e, st6=st6, mv=mv: e.bn_aggr(out=mv, in_=st6), [kN], [kN])

                def sB(i):
                    P.tag = 'sB(' + str(i) + ')'
                    tsl = slice(i * 128, (i + 1) * 128)
                    S0, S2_, kid, bi = tslots(i)
                    ncol = 257 if mx == 0 else 256
                    vx = v_ext[bi][:, 0:ncol]
                    bN = newbank()
                    mm(bN[1][:, 0:ncol], PT[bi], vx, True, False, [('PT', bi), ('v', bi), ('vone', bi)], [bN[2]])
                    for c in range(2):
                        mm(bN[1][:, 0:ncol], qT[:, c, tsl], Sbp[:, c, 0:ncol], False, c == 1, qkkeys(i) + ['Sbp'], [bN[2]])
                    bd = [newbank() for _ in range(2)]
                    for c in range(2):
                        mm(bd[c][1][:, 0:ncol], kw[bi][:, c * 128:(c + 1) * 128], vx, True, True,
                           [('kw', bi), ('v', bi), ('vone', bi)], [bd[c][2]])
                    scale = sclbc[:, h, i:i + 1] if mx == 0 else gam ** 128
                    for c in range(2):
                        stt(Spr[:, c, 0:ncol], Spr[:, c, 0:ncol], scale, bd[c][1][:, 0:ncol], ALU.mult, ALU.add,
                            ['Spr', bd[c][2], 'sclbc'], ['Spr'])
                    cp('act', Sbp[:, :, 0:257], Spr[:], ['Spr'], ['Sbp'])
                    readout_tail(i, bN)
                    if i == 15:
                        od = C_p if mx == 0 else S_p
                        P.dma('sp', lambda e, od=od, h=h: e.dma_start(out=od[h].rearrange("(c p) v -> p c v", p=128),
                                                                      in_=Spr[:, :, 0:256]),
                              reads=['Spr'], writes=(), sem='ds_Spr')
                        if mx == 0:
                            cp('dve', nP.rearrange("p (c j) -> p c j", c=2)[:, :, h:h + 1], Spr[:, :, 256:257], ['Spr'], ['nP'])

                def samp_start():
                    P.tag = 'samp'
                    ncol = 257 if mx == 0 else 256
                    for c in range(2):
                        src_blk = qT[:, c, TP:T].rearrange("p (s k) -> p s k", s=SEQS)
                        dst_blk = bass.AP(qblk.tensor, qblk[:, c, 0, 0:1].offset, [list(qblk.ap[0]), [136, 16], [1, 8]])
                        cp('pool', dst_blk, src_blk, qkkeys(16), ['qblk'])
                    bN = newbank()
                    pinned.add(bN[0])
                    samp_bank['bN'] = bN
                    mm(bN[1][:, 0:ncol], PT[2], v_ext[2][:, 0:ncol], True, False, [('PT', 2), ('v', 2), ('vone', 2)], [bN[2]])

                def seq_load(sq_):
                    sb_ = sq_ % 3
                    load(Sld[sb_], st_dram[sq_, h].rearrange("(c p) v -> p c v", p=128), [('S', sb_)])

                def seq_cast(sq_):
                    sb_ = sq_ % 3
                    cp('act', Sbs[sb_][:, :, 0:256], Sld[sb_], [('S', sb_)], [('Sb', sb_)])
                    if mx == 0:
                        cp('pool', Sbs[sb_][:, :, 256:257], nT[:, :, sq_ * 4 + h:sq_ * 4 + h + 1], ['nT'], [('Sb', sb_)])

                def seq_step(sq_):
                    P.tag = 'seq(' + str(sq_) + ')'
                    sb_ = sq_ % 3
                    ncol = 257 if mx == 0 else 256
                    vx = v_ext[2][:, 0:ncol]
                    bN = samp_bank['bN']
                    for c in range(2):
                        mm(bN[1][:, 0:ncol], qblk[:, c, sq_, :], Sbs[sb_][:, c, 0:ncol], False,
                           sq_ == SEQS - 1 and c == 1, ['qblk', ('Sb', sb_)], [bN[2]])
                    bd = [newbank() for _ in range(2)]
                    for c in range(2):
                        ts('pool', kwblk[:, c, :], kw[2][:, c * 128:(c + 1) * 128], rowmask[:, sq_:sq_ + 1], 1.0,
                           ALU.mult, ALU.mult, [('kw', 2), 'rowmask'], [('kwb', c)])
                        mm(bd[c][1][:, 0:ncol], kwblk[:, c, :], vx, True, True,
                           [('kwb', c), ('v', 2), ('vone', 2)], [bd[c][2]])
                    scale = sclbc[:, h, 16 + sq_:17 + sq_] if mx == 0 else gam ** 8
                    for c in range(2):
                        stt(Sld[sb_][:, c, :], Sld[sb_][:, c, :], scale, bd[c][1][:, 0:256], ALU.mult, ALU.add,
                            [('S', sb_), bd[c][2], 'sclbc'], [('S', sb_)])
                        if mx == 0:
                            stt(nnewT[:, c, sq_ * 4 + h:sq_ * 4 + h + 1], nT[:, c, sq_ * 4 + h:sq_ * 4 + h + 1], scale,
                                bd[c][1][:, 256:257], ALU.mult, ALU.add, ['nT', bd[c][2], 'sclbc'], ['nnewT'])
                    P.dma('sp', lambda e, sq_=sq_, sb_=sb_, h=h, out_dram=out_dram: e.dma_start(
                        out=out_dram[sq_, h].rearrange("(c p) v -> p c v", p=128), in_=Sld[sb_]),
                        reads=[('S', sb_)], writes=(), sem='ds_S%d' % sb_)

                def sC(i):
                    P.tag = 'sC(' + str(i) + ')'
                    samp = (i == 16)
                    S0, S2_, kid, bi = tslots(i)
                    ksz = ('tmp', 6) if samp else ('sz', kid)
                    kN = ('tmp', 7) if samp else ('N', kid)
                    mv = S2_[:, 264:266]
                    sdv = S2_[:, 266:267]
                    rsv = S2_[:, 267:268]
                    hc = S2_[:, 0:256]
                    if mx == 0:
                        epsv = epsc[:, 0:1]
                    else:
                        epsv = rcols[:, (20 if samp else 16) + h:(20 if samp else 16) + h + 1]
                    ts('dve', sdv, mv[:, 1:2], epsv, None, ALU.add, None, [kN, 'epsc', 'rcols'], [kN])
                    tt('pool', rsv, sdv, mhalf[:, 0:1], ALU.pow, [kN, 'mhalf'], [kN])
                    ts('dve', hc, hc, mv[:, 0:1], rsv, ALU.subtract, ALU.mult, [kN], [kN])
                    tt('pool', y_tok[i % 2], hc, S0[:, 256:512], ALU.mult, [kN, ksz], [('yt', i % 2)])

                ybanks = {}

                def sC_pe(i):
                    P.tag = 'sC_pe(' + str(i) + ')'
                    par = i % 2
                    by = newbank()
                    byb = by[1].bitcast(BF16)
                    for c in range(2):
                        tr(byb[:, c * 128:(c + 1) * 128], y_tok[par][:, c * 128:(c + 1) * 128], identB[:], [('yt', par), 'identB'], [by[2]])
                    ybanks[i] = (by, byb)

                def sD(i):
                    P.tag = 'sD(' + str(i) + ')'
                    tsl = slice(i * 128, (i + 1) * 128)
                    by, byb = ybanks.pop(i)
                    if ui == 0 and i == 0:
                        P.fence('act', [('r', k_) for k_ in range(8)])
                    for c in range(2):
                        act(ybig[:, 2 * h + c, tsl], byb[:, c * 128:(c + 1) * 128], AF.Copy, [by[2], 'gcm1'],
                            [('y', mx * 8 + 2 * h + c, i)], scale=gcm1[:, mx * 8 + 2 * h + c:mx * 8 + 2 * h + c + 1])

                NP_ = 16
                for sq_ in range(3):
                    seq_load(sq_)
                sA(16)
                samp_start()
                seq_cast(0)
                sA(0)
                for j in range(NP_ + 2):
                    if 0 <= j - 2 < NP_:
                        sD(j - 2)
                    if 0 <= j - 1 < NP_:
                        sC(j - 1)
                    if j + 1 < NP_:
                        sA(j + 1)
                    if j < NP_:
                        sB(j)
                    if 0 <= j - 1 < NP_:
                        sC_pe(j - 1)
                    if j < SEQS:
                        seq_step(j)
                        if j + 3 < SEQS:
                            seq_load(j + 3)
                        if j + 1 < SEQS:
                            seq_cast(j + 1)
                pinned.clear()
                readout_tail(16, samp_bank['bN'])
                sC(16)
                sC_pe(16)
                sD(16)
                if L1_STOP[0] == 6 or L1_STOP[0] == 10 + ui:
                    raise _Stop()
                if ui + 1 < len(units):
                    load_voz(*units[ui + 1])
                if mx == 0 and h == 3:
                    b, bap, bk = newbank()
                    for c in range(2):
                        tr(bap[0:4, c * 128:(c + 1) * 128], nP.rearrange("p (c j) -> p c j", c=2)[:, c, :], identF[:], ['nP', 'identF'], [bk])
                    cp('act', stg[0][0:4, :], bap[0:4, 0:128], [bk], [('stg', 0)])
                    cp('act', stg[1][0:4, :], bap[0:4, 128:256], [bk], [('stg', 1)])
                    store(n_p[:, 0:128], stg[0][0:4, :], [('stg', 0)])
                    store(n_p[:, 128:256], stg[1][0:4, :], [('stg', 1)])
                    b, bap, bk = newbank()
                    for c in range(2):
                        tr(bap[0:64, c * 128:(c + 1) * 128], nnewT[:, c, :], identF[:], ['nnewT', 'identF'], [bk])
                    cp('act', stg[0][0:64, :], bap[0:64, 0:128], [bk], [('stg', 0)])
                    cp('act', stg[1][0:64, :], bap[0:64, 128:256], [bk], [('stg', 1)])
                    store(n_s[:, 0:128], stg[0][0:64, :], [('stg', 0)])
                    store(n_s[:, 128:256], stg[1][0:64, :], [('stg', 1)])

            P.barrier()
            P.mark('L1_out')
            R1.reset()
            wout1 = R1.alloc((16, D), BF16)
            for q_ in range(4):
                load_flat(wout1[:, q_ * 4:(q_ + 1) * 4, :].rearrange("p k n -> p (k n)"), WO1_d[:, q_ * 4096:(q_ + 1) * 4096], [('wout', q_)])
            out_phase(1, wout1, y_out, False)
          except _Stop:
            pass

        P.mark('end')
        P.wait_all_dma('sp')
        _CACHE['marks'] = P.marks
        _CACHE['labels'] = P.labels
        P.emit()
    return nc


_CACHE = {}
LAYERS = 2


def _consts():
    c = {}
    c["identF"] = np.eye(128, dtype=np.float32)
    c["identB"] = np.eye(128, dtype=np.float32).astype(ml_dtypes.bfloat16)
    half = 128
    inv = (np.float32(10000.0) ** (-np.arange(half, dtype=np.float32) / np.float32(half))).astype(np.float32)
    pos = np.concatenate([np.arange(TP, dtype=np.float32),
                          np.tile(np.float32(16384.0) + np.arange(8, dtype=np.float32), SEQS)]).astype(np.float32)
    ang = (pos[None, :] * inv[:, None]).astype(np.float32)
    c["cs_tab"] = np.stack([np.cos(ang), np.sin(ang)]).astype(np.float32)
    p = np.arange(128)
    s_, t_ = p[:, None], p[None, :]
    causal = s_ <= t_
    same = (s_ // 8) == (t_ // 8)
    c["maskadd"] = np.stack([np.where(causal, 1.0, 0.0), np.where(causal & same, 1.0, 0.0)]).astype(np.float32)
    log_g = np.log(np.float32(1.0) - np.float32(2.0) ** (-5.0 - np.arange(4, dtype=np.float32))).astype(np.float32)
    rc = np.zeros((128, 24), np.float64)
    lg = log_g.astype(np.float64)
    for h in range(4):
        rc[:, 0 + h] = np.exp(-lg[h] * (p + 1.0))
        rc[:, 4 + h] = np.exp(-lg[h] * ((p % 8) + 1.0))
        rc[:, 8 + h] = np.exp(lg[h] * (127.0 - p))
        rc[:, 12 + h] = np.exp(lg[h] * (7.0 - (p % 8)))
        rc[:, 16 + h] = EPS * np.exp(-2.0 * lg[h] * (p + 1.0))
        rc[:, 20 + h] = EPS * np.exp(-2.0 * lg[h] * ((p % 8) + 1.0))
    c["rcols"] = rc.astype(np.float32)
    c["rowmask"] = ((p[:, None] // 8) == np.arange(16)[None, :]).astype(np.float32)
    c["rowsel"] = (p[:, None] == 8 * np.arange(16)[None, :]).astype(np.float32)
    sc = np.zeros((4, 2, 128), np.float32)
    sc[:, 0, :] = np.where(p % 8 == 0, 0.0, 1.0)[None, :]
    sc[:, 1, :] = np.where(p % 8 == 0, -BIG, 0.0)[None, :]
    c["scanc"] = sc
    return c


def kernel(x_prompt, x_sample, state_a_conv, state_b_conv, state_c_C, state_c_n, state_c_m, state_d_S,
           norm_pre, norm_post, w_in_conv, a_conv_w, b_conv_w, b_conv_b, b_ln_g, b_ln_b, w_out_conv,
           w_in_rec, c_i_b, c_f_b, c_hn_g, d_hn_g, w_out_rec):
    f = lambda a: np.ascontiguousarray(np.asarray(a, dtype=np.float32))
    x_prompt, x_sample = f(x_prompt), f(x_sample)
    state_a_conv, state_b_conv = f(state_a_conv), f(state_b_conv)
    state_c_C, state_c_n, state_c_m, state_d_S = f(state_c_C), f(state_c_n), f(state_c_m), f(state_d_S)
    if 'nc' not in _CACHE:
        _CACHE['nc'] = build_program(layers=LAYERS)
    nc = _CACHE['nc']
    cst = _consts()
    cvec = np.concatenate([f(norm_pre), f(a_conv_w)[0], f(b_conv_w)[0], f(b_conv_b), f(b_ln_g), f(b_ln_b)], axis=0)
    assert cvec.shape == (NPAR, D)
    common = {
        "cvec": cvec, "npost": f(norm_post),
        "identF": cst["identF"], "identB": cst["identB"],
    }
    w5 = f(w_in_conv)[0].reshape(8, 128, 7, 8, 128).transpose(3, 1, 0, 2, 4)
    common["WA"] = np.ascontiguousarray(w5[:, :, :, 0:4, :]).reshape(8, 128, 4096)
    common["WB"] = np.ascontiguousarray(w5[:, :, :, 4:7, :]).reshape(8, 128, 3072)
    common["WO0"] = np.ascontiguousarray(f(w_out_conv)[0].reshape(16, 128, D).transpose(1, 0, 2)).reshape(128, 16 * D)
    if LAYERS > 1:
        wrec = f(w_in_rec)[0]
        w6 = wrec[:, :9216].reshape(8, 128, 9, 4, 256).transpose(2, 3, 1, 0, 4)
        common["WQ"] = np.ascontiguousarray(np.concatenate([w6[0], w6[5]], axis=0)).reshape(8, 128, 2048)
        common["WK"] = np.ascontiguousarray(np.concatenate([w6[1], w6[6]], axis=0)).reshape(8, 128, 2048)
        common["WVc"] = np.ascontiguousarray(np.concatenate([w6[3], w6[4], w6[2]], axis=-1)).reshape(4, 128, 8 * 768)
        common["WVd"] = np.ascontiguousarray(np.concatenate([w6[8], w6[7]], axis=-1)).reshape(4, 128, 8 * 512)
        common["WG"] = np.ascontiguousarray(wrec[:, 9216:9224].reshape(8, 128, 8).transpose(1, 0, 2)).reshape(128, 64)
        common["WO1"] = np.ascontiguousarray(f(w_out_rec)[0].reshape(16, 128, D).transpose(1, 0, 2)).reshape(128, 16 * D)
        common.update({
            "cib": f(c_i_b)[0].reshape(4, 1), "cfb": f(c_f_b)[0].reshape(4, 1),
            "hng": np.concatenate([f(c_hn_g), f(d_hn_g)], axis=0),
        })
        for k in ("cs_tab", "maskadd", "rcols", "rowmask", "rowsel", "scanc"):
            common[k] = cst[k]
    in_maps = []
    for i in range(NCORES):
        sl = slice(SEQS * i, SEQS * (i + 1))
        m = dict(common)
        m["x_all"] = np.concatenate([x_prompt[i], x_sample[sl].reshape(128, D)], axis=0)
        m["st_a"] = state_a_conv[0, sl].reshape(SEQS * 2, D)
        m["st_b"] = state_b_conv[0, sl].reshape(SEQS * 30, D)
        if LAYERS > 1:
            m["st_C"] = state_c_C[0, sl]
            m["st_S"] = state_d_S[0, sl]
            m["st_n"] = state_c_n[0, sl].reshape(SEQS * 4, 256)
            m["st_m"] = state_c_m[0, sl]
        in_maps.append(m)
    res = run_bass_kernel_spmd(nc, in_maps, core_ids=list(range(NCORES)))
    R = res.results
    y = np.stack([r["y_out"] for r in R])
    y_p = np.ascontiguousarray(y[:, :TP, :])
    y_s = np.ascontiguousarray(y[:, TP:, :]).reshape(128, 8, D)
    a_p = np.stack([r["a_p"] for r in R])[None]
    a_s = np.concatenate([r["a_s"].reshape(SEQS, 2, D) for r in R])[None]
    b_p = np.stack([r["b_p"] for r in R])[None]
    b_s = np.concatenate([r["b_s"] for r in R])[None]
    if LAYERS == 1:
        z = lambda *s: np.zeros(s, np.float32)
        return (y_p, y_s, a_p, a_s, b_p, b_s,
                z(1, 8, 4, 256, 256), z(1, 128, 4, 256, 256), z(1, 8, 4, 256), z(1, 128, 4, 256),
                z(1, 8, 4), z(1, 128, 4), z(1, 8, 4, 256, 256), z(1, 128, 4, 256, 256))
    C_p = np.stack([r["C_p"] for r in R])[None]
    C_s = np.concatenate([r["C_s"] for r in R])[None]
    n_p = np.stack([r["n_p"] for r in R])[None]
    n_s = np.concatenate([r["n_s"].reshape(SEQS, 4, 256) for r in R])[None]
    m_p = np.stack([r["m_p"].reshape(4) for r in R])[None]
    m_s = np.concatenate([r["m_s"] for r in R])[None]
    S_p = np.stack([r["S_p"] for r in R])[None]
    S_s = np.concatenate([r["S_s"] for r in R])[None]
    return (y_p, y_s, a_p, a_s, b_p, b_s, C_p, C_s, n_p, n_s, m_p, m_s, S_p, S_s)
```
